# Optimizing a Trainium2 kernel written in Bass

```python
import jax, jax.numpy as jnp
from jax import lax
import numpy as np

D_MODEL = 2048
BATCH = 2
SEQ = 16384
DEPTH = 1

D_MIX = D_MODEL
GDN_DK = 128
GDN_DV = 128
GDN_HEADS = (D_MIX // 2) // GDN_DV
GDN_CHUNK = 64
CONV_K = 5
HGRN_DK = 128
HGRN_DV = 128
HGRN_HEADS = (D_MIX - GDN_HEADS * GDN_DV) // HGRN_DV
HGRN_CHUNK = 32
D_FF = -(-8 * D_MODEL // (3 * 256)) * 256
NORM_EPS = 1e-6

GA_QK = GDN_HEADS * GDN_DK
GA_V = GDN_HEADS * GDN_DV
GDN_CONV_CH = 2 * GA_QK + GA_V
HB_K = HGRN_HEADS * HGRN_DK
HB_V = HGRN_HEADS * HGRN_DV
SPLIT_SIZES = (GDN_CONV_CH, GA_V, 2 * GDN_HEADS, 2 * GDN_HEADS, HB_K, 2 * HB_K, HB_V, HB_V)
D_IN = GDN_CONV_CH + GA_V + 4 * GDN_HEADS + 3 * HB_K + 2 * HB_V

kernel_name = "hybrid_gdn_hgrn2_parallel_bidir_block"


def _rmsnorm(x, w):
    xf = x.astype(jnp.float32)
    y = xf * lax.rsqrt(jnp.mean(xf * xf, axis=-1, keepdims=True) + NORM_EPS)
    return (y * w.astype(jnp.float32)).astype(x.dtype)


def _l2norm(x):
    return x * lax.rsqrt(jnp.sum(x * x, axis=-1, keepdims=True) + NORM_EPS)


def _split_cols(t):
    outs, off = [], 0
    for size in SPLIT_SIZES:
        outs.append(t[..., off:off + size])
        off += size
    return outs


def _heads(t, n_heads):
    b, s, _ = t.shape
    return t.reshape(b, s, n_heads, -1).transpose(0, 2, 1, 3)


def _merge_heads(t):
    b, h, s, d = t.shape
    return t.transpose(0, 2, 1, 3).reshape(b, s, h * d)


def _dir_params(t, n_heads):
    b, s, _ = t.shape
    return t.reshape(b, s, 2, n_heads).transpose(2, 0, 3, 1)


def _short_conv(x, w):
    c = x.shape[-1]
    return lax.conv_general_dilated(
        x, w[:, None, :].astype(x.dtype), window_strides=(1,),
        padding=[((CONV_K - 1) // 2, CONV_K // 2)],
        dimension_numbers=("NWC", "WIO", "NWC"), feature_group_count=c)


def _gdn_scan(q, k, v, g, beta):
    b_, h_, s_, dk = q.shape
    dv = v.shape[-1]
    c = GDN_CHUNK
    n = s_ // c
    q = q.reshape(b_, h_, n, c, dk)
    k = k.reshape(b_, h_, n, c, dk)
    v = v.reshape(b_, h_, n, c, dv)
    g = g.reshape(b_, h_, n, c)
    beta = beta.reshape(b_, h_, n, c)
    G = jnp.cumsum(g, axis=-1)
    incl = jnp.tril(jnp.ones((c, c), dtype=bool))
    strict = jnp.tril(jnp.ones((c, c), dtype=bool), -1)
    decay = jnp.exp(jnp.where(incl, G[..., :, None] - G[..., None, :], -jnp.inf))
    L = beta[..., None] * jnp.einsum("bhnrd,bhnjd->bhnrj", k, k) * jnp.where(strict, decay, 0.0)
    gamma = jnp.exp(G)
    rhs = jnp.concatenate([beta[..., None] * v, (beta * gamma)[..., None] * k], axis=-1)
    sol = lax.linalg.triangular_solve(jnp.eye(c, dtype=q.dtype) + L, rhs,
                                      left_side=True, lower=True, unit_diagonal=True)
    U, Wk = sol[..., :dv], sol[..., dv:]
    Aqk = jnp.einsum("bhnrd,bhnjd->bhnrj", q, k) * decay
    q_dec = q * gamma[..., None]
    k_dec = k * jnp.exp(G[..., -1:] - G)[..., None]
    g_last = jnp.exp(G[..., -1])

    def step(S0, xs):
        u_c, wk_c, a_c, qd_c, kd_c, gl_c = xs
        w = u_c - jnp.einsum("bhrk,bhkv->bhrv", wk_c, S0)
        o = jnp.einsum("bhrk,bhkv->bhrv", qd_c, S0) + jnp.einsum("bhrj,bhjv->bhrv", a_c, w)
        S1 = gl_c[..., None, None] * S0 + jnp.einsum("bhjk,bhjv->bhkv", kd_c, w)
        return S1, o

    xs = tuple(jnp.moveaxis(t, 2, 0) for t in (U, Wk, Aqk, q_dec, k_dec, g_last))
    S0 = jnp.zeros((b_, h_, dk, dv), q.dtype)
    _, o = lax.scan(step, S0, xs)
    return jnp.moveaxis(o, 0, 2).reshape(b_, h_, s_, dv)


def _hgrn2_scan(q, v, k, logf):
    b_, h_, s_, dk = q.shape
    dv = v.shape[-1]
    c = HGRN_CHUNK
    n = s_ // c
    to_chunks = lambda t: jnp.moveaxis(t.reshape(b_, h_, n, c, t.shape[-1]), 2, 0)
    incl = jnp.tril(jnp.ones((c, c), dtype=bool))[..., None]

    def step(S0, xs):
        q_c, k_c, v_c, lf_c = xs
        Bc = jnp.cumsum(lf_c, axis=-2)
        dec = jnp.exp(jnp.where(incl, Bc[..., :, None, :] - Bc[..., None, :, :], -jnp.inf))
        A = jnp.sum(q_c[..., :, None, :] * k_c[..., None, :, :] * dec, axis=-1)
        o = (jnp.einsum("bhrk,bhkv->bhrv", q_c * jnp.exp(Bc), S0)
             + jnp.einsum("bhrj,bhjv->bhrv", A, v_c))
        S1 = (jnp.exp(Bc[..., -1, :])[..., None] * S0
              + jnp.einsum("bhjk,bhjv->bhkv", k_c * jnp.exp(Bc[..., -1:, :] - Bc), v_c))
        return S1, o

    S0 = jnp.zeros((b_, h_, dk, dv), q.dtype)
    _, o = lax.scan(step, S0, (to_chunks(q), to_chunks(k), to_chunks(v), to_chunks(logf)))
    return jnp.moveaxis(o, 0, 2).reshape(b_, h_, s_, dv)


def _bidir(scan_fn, shared, fwd, bwd):
    flip = lambda t: jnp.flip(t, axis=2)
    o_f = scan_fn(*shared, *fwd)
    o_b = scan_fn(*[flip(t) for t in shared], *[flip(t) for t in bwd])
    return o_f + flip(o_b)


def setup_inputs(seed: int = 0) -> dict:
    key = jax.random.key(seed)
    ks = jax.random.split(key, 16)
    f32 = jnp.float32
    nrm = lambda k, shape, fan_in: jax.random.normal(k, shape, f32) * fan_in ** -0.5
    gain = lambda k, shape: 1.0 + 0.02 * jax.random.normal(k, shape, f32)
    dt = jnp.exp(jax.random.uniform(ks[5], (DEPTH, 2, GDN_HEADS), f32,
                                    np.log(1e-3).astype(np.float32), np.log(0.1).astype(np.float32)))
    return {
        "x": jax.random.normal(ks[0], (BATCH, SEQ, D_MODEL), f32),
        "norm1_w": gain(ks[1], (DEPTH, D_MODEL)),
        "w_in": nrm(ks[2], (DEPTH, D_MODEL, D_IN), D_MODEL),
        "conv_w": nrm(ks[3], (DEPTH, CONV_K, GDN_CONV_CH), CONV_K),
        "gdn_a_log": jnp.log(jax.random.uniform(ks[4], (DEPTH, 2, GDN_HEADS), f32, 1.0, 16.0)),
        "gdn_dt_bias": dt + jnp.log(-jnp.expm1(-dt)),
        "gdn_norm_w": gain(ks[6], (DEPTH, GDN_DV)),
        "hgrn_lb_logits": 0.1 * jax.random.normal(ks[7], (DEPTH + 1, 2, HB_K), f32),
        "hgrn_norm_w": gain(ks[8], (DEPTH, HGRN_DV)),
        "w_out": nrm(ks[9], (DEPTH, D_MIX, D_MODEL), D_MIX),
        "norm2_w": gain(ks[10], (DEPTH, D_MODEL)),
        "w_gate": nrm(ks[11], (DEPTH, D_MODEL, D_FF), D_MODEL),
        "w_up": nrm(ks[12], (DEPTH, D_MODEL, D_FF), D_MODEL),
        "w_down": nrm(ks[13], (DEPTH, D_FF, D_MODEL), D_FF),
        "norm_f_w": gain(ks[14], (D_MODEL,)),
    }


def reference(x, norm1_w, w_in, conv_w, gdn_a_log, gdn_dt_bias, gdn_norm_w, hgrn_lb_logits,
              hgrn_norm_w, w_out, norm2_w, w_gate, w_up, w_down, norm_f_w):
    f32 = jnp.float32
    lb_all = jnp.cumsum(jax.nn.softmax(hgrn_lb_logits.astype(f32), axis=0), axis=0)
    for l in range(DEPTH):
        h = _rmsnorm(x, norm1_w[l])
        proj = h @ w_in[l]
        qkv_a, z_a, beta_a, alpha_a, q_b, f_b, i_b, g_b = _split_cols(proj)

        qkv_a = jax.nn.silu(_short_conv(qkv_a, conv_w[l])).astype(f32)
        q_a = _l2norm(_heads(qkv_a[..., :GA_QK], GDN_HEADS)) * GDN_DK ** -0.5
        k_a = _l2norm(_heads(qkv_a[..., GA_QK:2 * GA_QK], GDN_HEADS))
        v_a = _heads(qkv_a[..., 2 * GA_QK:], GDN_HEADS)
        beta = jax.nn.sigmoid(_dir_params(beta_a.astype(f32), GDN_HEADS))
        a_raw = _dir_params(alpha_a.astype(f32), GDN_HEADS)
        g = (-jnp.exp(gdn_a_log[l].astype(f32))[:, None, :, None]
             * jax.nn.softplus(a_raw + gdn_dt_bias[l].astype(f32)[:, None, :, None]))
        o_a = _bidir(_gdn_scan, (q_a, k_a, v_a), (g[0], beta[0]), (g[1], beta[1]))
        z_h = _heads(z_a.astype(f32), GDN_HEADS)
        y_a = _merge_heads(_rmsnorm(o_a, gdn_norm_w[l]) * jax.nn.silu(z_h))

        lb = lb_all[l]
        f_raw = f_b.astype(f32).reshape(f_b.shape[0], f_b.shape[1], 2, HB_K)
        f = lb + (1.0 - lb) * jax.nn.sigmoid(f_raw)
        f_fw, f_bw = _heads(f[:, :, 0], HGRN_HEADS), _heads(f[:, :, 1], HGRN_HEADS)
        qh_b = _heads(jax.nn.silu(q_b.astype(f32)), HGRN_HEADS)
        ih_b = _heads(i_b.astype(f32), HGRN_HEADS)
        o_b = _bidir(_hgrn2_scan, (qh_b, ih_b),
                     (1.0 - f_fw, jnp.log(f_fw)), (1.0 - f_bw, jnp.log(f_bw)))
        gh_b = _heads(g_b.astype(f32), HGRN_HEADS)
        y_b = _merge_heads(_rmsnorm(o_b, hgrn_norm_w[l]) * jax.nn.sigmoid(gh_b))

        y = jnp.concatenate([y_a, y_b], axis=-1).astype(x.dtype)
        x = x + y @ w_out[l]

        h2 = _rmsnorm(x, norm2_w[l])
        x = x + (jax.nn.silu(h2 @ w_gate[l]) * (h2 @ w_up[l])) @ w_down[l]
    return _rmsnorm(x, norm_f_w)
```

```python
import os
import numpy as np
import ml_dtypes
STAGE = float(os.environ.get('K_STAGE', '99'))
NOCC = os.environ.get('K_NOCC', '0') == '1'
NOCC2 = NOCC or os.environ.get('K_NOCC2', '0') == '1'
from contextlib import ExitStack
import concourse.bass as bass
import concourse.mybir as mybir
from concourse.bass_utils import run_bass_kernel_spmd

F32 = mybir.dt.float32
BF = mybir.dt.bfloat16
ALU = mybir.AluOpType
AF = mybir.ActivationFunctionType

D = 2048
DC = 16
DFF = 5632
FC = 44
EPS = 1e-6
NCB = 16
NCOL = 18 * 128 + 8


class Cell:
    __slots__ = ("sem", "cnt")

    def __init__(self, sem):
        self.sem = sem
        self.cnt = 0


class Buf:
    __slots__ = ("w", "r", "d", "name")

    def __init__(self, name=""):
        self.w = None
        self.r = {}
        self.d = None
        self.name = name


class Eng:
    def __init__(self, name, sem):
        self.name = name
        self.sem = sem
        self.cnt = 0
        self.waited = {}
        self.prog = []


class KB:
    def __init__(self, nc, stack):
        self.nc = nc
        self.st = stack
        self.E = {}
        for n in ("pe", "act", "dve", "pool", "sp"):
            self.E[n] = Eng(n, stack.enter_context(nc.semaphore("s_" + n)))
        self.cells = []
        self.free_cells = []
        self.phase_cells = []
        self.top = stack

    def sb(self, name, shape, dt):
        t = self.st.enter_context(self.nc.sbuf_tensor(name, list(shape), dt))
        return t

    def ps(self, name, shape, dt):
        return self.st.enter_context(self.nc.psum_tensor(name, list(shape), dt))

    def dbuf(self, name, sw=False):
        b = Buf(name)
        if self.free_cells and not sw:
            c = self.free_cells.pop()
        else:
            c = Cell(self.top.enter_context(self.nc.semaphore("d%d" % len(self.cells))))
            self.cells.append(c)
        b.d = c
        if not sw:
            self.phase_cells.append(c)
        return b

    def phase_end(self):
        self.barrier()
        self.free_cells.extend(self.phase_cells)
        self.phase_cells = []

    def _waits(self, E, reads, writes):
        need = {}

        def add(tok):
            if tok is None:
                return
            sem, val = tok
            k = id(sem)
            if k not in need or need[k][1] < val:
                need[k] = (sem, val)

        for b in reads:
            add(b.w)
        for b in writes:
            add(b.w)
            for t in b.r.values():
                add(t)
        out = []
        for k, (sem, val) in need.items():
            if sem is E.sem and E.name == "pe":
                continue
            if E.waited.get(k, 0) >= val:
                continue
            E.waited[k] = val
            out.append((sem, val))
        return out

    def op(self, en, fn, reads=(), writes=()):
        E = self.E[en]
        ws = self._waits(E, reads, writes)
        E.cnt += 1
        tok = (E.sem, E.cnt)
        fns = fn if isinstance(fn, (list, tuple)) else [fn]

        def run(e, ws=ws, fns=fns, sem=E.sem):
            for s, v in ws:
                e.wait_ge(s, v)
            for f in fns[:-1]:
                f(e)
            fns[-1](e).then_inc(sem, 1)

        E.prog.append(run)
        for b in reads:
            b.r[en] = tok
        for b in writes:
            b.w = tok
            b.r = {}
        return tok

    def dma(self, qn, out_ap, in_ap, owner, reads=(), writes=(), _dyn=None, **kw):
        E = self.E[qn]
        ws = self._waits(E, reads, writes)
        owner.d.cnt += 16
        tok = (owner.d.sem, owner.d.cnt)

        def run(e, ws=ws, sem=owner.d.sem):
            for s, v in ws:
                e.wait_ge(s, v)
            src = _dyn(e) if _dyn is not None else in_ap
            e.dma_start(out=out_ap, in_=src, **kw).then_inc(sem, 16)

        E.prog.append(run)
        key = "dma%d" % id(owner)
        for b in reads:
            b.r[key] = tok
        for b in writes:
            b.w = tok
            b.r = {}
        return tok

    def raw(self, en, fn):
        self.E[en].prog.append(fn)

    def barrier(self, engines=("pe", "act", "dve", "pool", "sp")):
        toks = []
        for E in self.E.values():
            if E.cnt:
                toks.append((E.sem, E.cnt))
        for o in self.cells:
            if o.cnt:
                toks.append((o.sem, o.cnt))
        for en in engines:
            E = self.E[en]
            ws = []
            for sem, val in toks:
                if sem is E.sem:
                    continue
                k = id(sem)
                if E.waited.get(k, 0) >= val:
                    continue
                E.waited[k] = val
                ws.append((sem, val))

            def run(e, ws=ws):
                for s, v in ws:
                    e.wait_ge(s, v)

            E.prog.append(run)

    def emit(self):
        nc = self.nc
        with nc.Block() as block:
            @block.sync
            def _(e):
                for f in self.E["sp"].prog:
                    f(e)

            @block.tensor
            def _(e):
                for f in self.E["pe"].prog:
                    f(e)

            @block.scalar
            def _(e):
                for f in self.E["act"].prog:
                    f(e)

            @block.vector
            def _(e):
                for f in self.E["dve"].prog:
                    f(e)

            @block.gpsimd
            def _(e):
                for f in self.E["pool"].prog:
                    f(e)


def build(S, debug=False, phases="WA2BNC"):
    NT = S // 128
    SQ = S // 4
    nc = bass.Bass("TRN2", target_bir_lowering=False)
    dkind = dict(kind="ExternalOutput") if debug else {}

    def din(name, shape, dt=F32):
        return nc.dram_tensor(name, list(shape), dt, kind="ExternalInput")

    def dscr(name, shape, dt=F32):
        if debug and (debug is True or name in debug):
            return nc.dram_tensor(name, list(shape), dt, kind="ExternalOutput")
        return nc.dram_tensor(name, list(shape), dt)

    xb = din("xb", [S, D])
    xq = din("xq", [SQ, D])
    win = din("win", [128, DC * NCOL])
    n1w = din("n1w", [128, DC])
    n2w = din("n2w", [128, DC])
    nfw = din("nfw", [128, DC])
    convw = din("convw", [128, 6 * 5])
    gsm = din("gsm", [4, 2])
    gnw = din("gnw", [128, 1])
    hnw = din("hnw", [128, 1])
    lbl = din("lbl", [128, 8])
    ident = din("ident", [128, 128])
    masks = din("masks", [128, 10 * 128])
    sel4 = din("sel4", [4, 4 * 128])
    wo_h = din("wo_h", [256, DC * 128])
    wg_h = din("wg_h", [FC * 16, DC * 128])
    wu_h = din("wu_h", [FC * 16, DC * 128])
    wd_h = din("wd_h", [256, FC * 128])
    out = nc.dram_tensor("out", [SQ, D], F32, kind="ExternalOutput")

    projcm = dscr("projcm", [NCB, 128, S + 4])
    vtok = dscr("vtok", [S, 256])
    smallT = dscr("smallT", [8, S])
    wsh = [nc.dram_tensor("wsh%d" % i, [r, c], BF) for i, (r, c) in
           enumerate(((256, DC * 128), (FC * 16, DC * 128), (FC * 16, DC * 128), (256, FC * 128)))]
    wo_b = nc.dram_tensor("wo_b", [16 * 128, DC * 128], BF)
    wg_b = nc.dram_tensor("wg_b", [FC * 128, DC * 128], BF)
    wu_b = nc.dram_tensor("wu_b", [FC * 128, DC * 128], BF)
    wd_b = nc.dram_tensor("wd_b", [16 * 128, FC * 128], BF)

    gops = dscr("gops", [4, NT, 128, 768], BF)
    hops = dscr("hops", [4, NT, 128, 512], BF)
    oT = dscr("oT", [8, 128, S])
    y_in = dscr("y_in", [512, S], BF)
    y_all = dscr("y_all", [8 * 512, S], BF)

    with ExitStack() as top:
        kb = KB(nc, top)
        ccs = top.enter_context(nc.semaphore("ccs"))
        ccs2 = top.enter_context(nc.semaphore("ccs2"))
        glres = [kb.sb("glres%d" % s_, [128, 2 * NT], F32) for s_ in range(8)]
        res_b = Buf("glres")

        def phase(fn, *args):
            with ExitStack() as st:
                kb.st = st
                fn(nc, kb, *args)
                kb.phase_end()
            kb.st = top

        def phase_w():
            if "W" in phases:
                wown = kb.dbuf("wcast", sw=True)
                for src, dst in zip((wo_h, wg_h, wu_h, wd_h), wsh):
                    kb.dma("pool", dst[:, :], src[:, :], wown, max_dma_last_dim=4096)
                wtok = (wown.d.sem, wown.d.cnt)

                def _ag(e):
                    e.wait_ge(wtok[0], wtok[1])
                    for i_, (src, dst) in enumerate(zip(wsh, (wo_b, wg_b, wu_b, wd_b))):
                        e.collective_compute("AllGather", ALU.bypass, replica_groups=[list(range(8))],
                                             ins=[src.ap().opt()], outs=[dst.ap().opt()]).then_inc(ccs)
                if not NOCC:
                    kb.raw("pool", _ag)
                else:
                    fk = kb.dbuf("fake")
                    kb.raw("sp", lambda e: e.wait_ge(wtok[0], wtok[1]))
                    for src, dst in zip(wsh, (wo_b, wg_b, wu_b, wd_b)):
                        n = src.shape[0]
                        for r8 in range(8):
                            kb.dma("sp", dst[r8 * n:(r8 + 1) * n, :], src[:, :], fk)


        wflag = Buf("wflag")
        flagt = kb.sb("flagt", [128, 2], F32)
        phase_w()
        if "A" in phases:
            phase(phase_a1, S, NT, xb, win, n1w, ident, projcm, vtok, smallT)
        if "2" in phases:
            phase(phase_a2, S, NT, projcm, vtok, smallT, ident, masks, sel4, convw, gsm, lbl, gops, hops, glres, res_b)
        if "B" in phases:
            phase(phase_b, S, NT, gops, hops, glres, res_b, oT)
        if "N" in phases:
            phase(phase_n, S, projcm, oT, gnw, hnw, y_in)

            def _ag2(e):
                e.collective_compute("AllGather", ALU.bypass, replica_groups=[list(range(8))],
                                     ins=[y_in.ap().opt()], outs=[y_all.ap().opt()]).then_inc(ccs2)
                e.wait_ge(ccs2, 1)
            if not NOCC2:
                kb.raw("pool", _ag2)
            else:
                fk2 = kb.dbuf("fake2")
                for r4 in range(8):
                    kb.dma("sp", y_all[r4 * 512:(r4 + 1) * 512, :], y_in[:, :], fk2)
                kb.barrier()
        if "C" in phases:
            if "W" in phases and not NOCC:
                kb.raw("pool", lambda e: e.wait_ge(ccs, 4))
            kb.op("pool", lambda e: e.memset(flagt[:, 0:1], 0.0), writes=[wflag])
            phase(phase_c, S, xq, y_all, wo_b, wg_b, wu_b, wd_b, n2w, nfw, ident, out, wflag)
        kb.barrier()
        kb.emit()
    return nc


def phase_a1(nc, kb, S, NT, xb, win, n1w, ident, projcm, vtok, smallT):
    wsb = kb.sb("wsb", [128, DC, NCOL], BF)
    wsb_b = kb.dbuf("wsb", sw=True)
    n1 = kb.sb("n1", [128, DC], F32)
    idf = kb.sb("idf", [128, 128], F32)
    idb = kb.sb("idb", [128, 128], BF)
    cst_b = kb.dbuf("cst")
    zpad = kb.sb("zpad", [128, NCB, 2], F32)
    zp_b = kb.dbuf("zp")
    xt = [kb.sb("xt%d" % i, [128, D], F32) for i in range(2)]
    xt_b = [kb.dbuf("xt%d" % i) for i in range(2)]
    junk = kb.sb("junk", [128, D], BF)
    junk_b = Buf()
    st_ = [kb.sb("st%d" % i, [128, 4], F32) for i in range(2)]
    st_b = [Buf() for _ in range(2)]
    hb = [kb.sb("hb%d" % i, [128, D], BF) for i in range(2)]
    hb_b = [Buf() for _ in range(2)]
    hT = [kb.sb("hT%d" % i, [128, DC, 128], BF) for i in range(2)]
    hT_b = [Buf() for _ in range(2)]
    ost = [kb.sb("ost%d" % i, [128, NCB, 128], F32) for i in range(2)]
    ost_b = [kb.dbuf("ost%d" % i) for i in range(2)]
    vst = [kb.sb("vst%d" % i, [128, 256], F32) for i in range(2)]
    vst_b = [kb.dbuf("vst%d" % i) for i in range(2)]
    sst = [kb.sb("sst%d" % i, [8, 128], F32) for i in range(2)]
    sst_b = [kb.dbuf("sst%d" % i) for i in range(2)]
    psT = [kb.ps("psT%d" % i, [128, 1024], BF) for i in range(2)]
    psT_b = [Buf() for _ in range(2)]
    psP = [kb.ps("psP%d" % i, [128, 512], F32) for i in range(4)]
    psP_b = [Buf() for _ in range(4)]
    psV = kb.ps("psV", [128, 512], F32)
    psV_b = Buf()

    for kc in range(DC):
        kb.dma("pool", wsb[:, kc, :], win[:, kc * NCOL:(kc + 1) * NCOL], wsb_b, writes=[wsb_b],
               max_dma_last_dim=4096)
    kb.dma("sp", n1[:], n1w[:, :], cst_b, writes=[cst_b])
    kb.dma("sp", idf[:], ident[:, :], cst_b, writes=[cst_b])
    kb.op("dve", lambda e: e.tensor_copy(out=idb[:], in_=idf[:]), reads=[cst_b], writes=[cst_b])
    kb.op("dve", lambda e: e.memset(zpad[:], 0.0), writes=[zp_b])
    pc = projcm.ap().rearrange("c p t -> p c t")
    kb.dma("sp", pc[:, :, 0:2], zpad[:], zp_b, reads=[zp_b])
    kb.dma("sp", pc[:, :, S + 2:S + 4], zpad[:], zp_b, reads=[zp_b])

    def load(i):
        kb.dma("sp", xt[i % 2][:], xb[i * 128:(i + 1) * 128, :], xt_b[i % 2], writes=[xt_b[i % 2]])

    def front_a(i):
        p = i % 2
        x_, xb_ = xt[p], xt_b[p]
        s_, sb_ = st_[p], st_b[p]
        kb.op("act", lambda e: e.activation(out=junk[:], in_=x_[:], func=AF.Square, accum_out=s_[:, 0:1]),
              reads=[xb_], writes=[junk_b, sb_])
        kb.op("act", lambda e: e.activation(out=s_[:, 1:2], in_=s_[:, 0:1], func=AF.Sqrt, scale=1.0 / D, bias=EPS),
              reads=[sb_], writes=[sb_])
        kb.op("dve", lambda e: e.reciprocal(out=s_[:, 2:3], in_=s_[:, 1:2]), reads=[sb_], writes=[sb_])
        h_, hb_ = hb[p], hb_b[p]
        kb.op("dve", lambda e: e.tensor_scalar(out=h_[:], in0=x_[:], scalar1=s_[:, 2:3], scalar2=None, op0=ALU.mult),
              reads=[xb_, sb_], writes=[hb_])

    def front_T(i):
        p = i % 2
        h_, hb_ = hb[p], hb_b[p]
        for half in range(2):
            pT, pTb = psT[half], psT_b[half]
            fns = []
            for j in range(8):
                kc = half * 8 + j
                fns.append(lambda e, j=j, kc=kc, pT=pT: e.transpose(
                    out=pT[:, j * 128:(j + 1) * 128], in_=h_[:, kc * 128:(kc + 1) * 128], identity=idb[:]))
            kb.op("pe", fns, reads=[hb_, cst_b], writes=[pTb])

    def front_S(i):
        p = i % 2
        t_, tb_ = hT[p], hT_b[p]
        for half in range(2):
            pT, pTb = psT[half], psT_b[half]
            kb.op("dve", lambda e, half=half, pT=pT: e.tensor_tensor(
                out=t_[:, half * 8:(half + 1) * 8, :],
                in0=pT[:].rearrange("p (k t) -> p k t", k=8),
                in1=n1[:, half * 8:(half + 1) * 8].unsqueeze(2).to_broadcast([128, 8, 128]),
                op=ALU.mult), reads=[pTb, cst_b], writes=[tb_])

    def back_pe(i):
        p = i % 2
        t_, tb_ = hT[p], hT_b[p]
        for g in range(4):
            fns = []
            for cbl in range(4):
                cb = g * 4 + cbl
                for kc in range(DC):
                    fns.append(lambda e, g=g, cbl=cbl, cb=cb, kc=kc: e.matmul(
                        out=psP[g][:, cbl * 128:(cbl + 1) * 128], lhsT=wsb[:, kc, cb * 128:(cb + 1) * 128],
                        rhs=t_[:, kc, :], start=(kc == 0), stop=(kc == DC - 1)))
            kb.op("pe", fns, reads=[tb_, wsb_b], writes=[psP_b[g]])
        fns = []
        for kc in range(DC):
            fns.append(lambda e, kc=kc: e.matmul(out=psV[:, 0:256], lhsT=t_[:, kc, :],
                                                 rhs=wsb[:, kc, 16 * 128:18 * 128],
                                                 start=(kc == 0), stop=(kc == DC - 1)))
        for kc in range(DC):
            fns.append(lambda e, kc=kc: e.matmul(out=psV[0:8, 256:384], lhsT=wsb[:, kc, 18 * 128:18 * 128 + 8],
                                                 rhs=t_[:, kc, :], start=(kc == 0), stop=(kc == DC - 1)))
        kb.op("pe", fns, reads=[tb_, wsb_b], writes=[psV_b])

    def back_ev(i):
        p = i % 2
        o_, ob_ = ost[p], ost_b[p]
        of = o_[:].rearrange("p c t -> p (c t)")
        kb.op("dve", lambda e: e.tensor_copy(out=of[:, 0:512], in_=psP[0][:]), reads=[psP_b[0]], writes=[ob_])
        kb.op("dve", lambda e: e.tensor_copy(out=of[:, 512:768], in_=psP[1][:, 0:256]), reads=[psP_b[1]],
              writes=[ob_])
        kb.op("act", lambda e: e.activation(out=of[:, 768:1024], in_=psP[1][:, 256:512], func=AF.Silu),
              reads=[psP_b[1]], writes=[ob_])
        kb.op("act", lambda e: e.activation(out=of[:, 1024:1280], in_=psP[2][:, 0:256], func=AF.Silu),
              reads=[psP_b[2]], writes=[ob_])
        kb.op("act", lambda e: e.activation(out=of[:, 1280:1536], in_=psP[2][:, 256:512], func=AF.Sigmoid),
              reads=[psP_b[2]], writes=[ob_])
        kb.op("act", lambda e: e.activation(out=of[:, 1536:2048], in_=psP[3][:], func=AF.Sigmoid),
              reads=[psP_b[3]], writes=[ob_])
        v_, vb_ = vst[p], vst_b[p]
        kb.op("dve", lambda e: e.tensor_copy(out=v_[:], in_=psV[:, 0:256]), reads=[psV_b], writes=[vb_])
        ss_, ssb_ = sst[p], sst_b[p]
        kb.op("dve", lambda e: e.tensor_copy(out=ss_[:], in_=psV[0:8, 256:384]), reads=[psV_b], writes=[ssb_])
        kb.dma("sp", pc[:, :, 2 + i * 128:2 + (i + 1) * 128], o_[:], ob_, reads=[ob_])
        kb.dma("sp", vtok[i * 128:(i + 1) * 128, :], v_[:], vb_, reads=[vb_])
        kb.dma("sp", smallT[:, i * 128:(i + 1) * 128], ss_[:], ssb_, reads=[ssb_])

    load(0)
    front_a(0)
    front_T(0)
    front_S(0)
    for i in range(NT):
        if i + 1 < NT:
            load(i + 1)
            front_a(i + 1)
        back_pe(i)
        if i + 1 < NT:
            front_T(i + 1)
        back_ev(i)
        if i + 1 < NT:
            front_S(i + 1)


class TB:
    def __init__(self, kb, name, shape, dt, dma=False, psum=False):
        self.t = (kb.ps if psum else kb.sb)(name, shape, dt)
        self.b = kb.dbuf(name) if dma else Buf(name)

    def __getitem__(self, k):
        return self.t[k]


def _bufs(xs):
    return [x.b if isinstance(x, TB) else x for x in xs]


def phase_a2(nc, kb, S, NT, projcm, vtok, smallT, ident, masks, sel4, convw, gsm, lbl, gops, hops, glres, res_b):
    def V(en, fn, R=(), W=()):
        kb.op(en, fn, reads=_bufs(R), writes=_bufs(W))

    mk = lambda name, shape, dt=F32, **k: TB(kb, name, shape, dt, **k)
    cst = mk("cst", [128, 128], F32, dma=True)
    ones = mk("ones", [128, 128], F32)
    msk = mk("msk", [128, 10, 128], F32, dma=True)
    sel = mk("sel", [4, 4, 128], F32, dma=True)
    cw = mk("cw", [128, 6, 5], F32, dma=True)
    gs = mk("gs", [4, 4], F32, dma=True)
    lbt = mk("lbt", [128, 12], F32, dma=True)
    pc = projcm.ap().rearrange("c p t -> p c t")

    kb.dma("sp", cst[:], ident[:, :], cst.b, writes=[cst.b])
    kb.dma("sp", msk[:].rearrange("p a b -> p (a b)"), masks[:, :], msk.b, writes=[msk.b])
    kb.dma("sp", sel[:].rearrange("p a b -> p (a b)"), sel4[:, :], sel.b, writes=[sel.b])
    kb.dma("sp", cw[:].rearrange("p a b -> p (a b)"), convw[:, :], cw.b, writes=[cw.b])
    kb.dma("sp", gs[:, 0:2], gsm[:, :], gs.b, writes=[gs.b])
    kb.dma("sp", lbt[:, 0:8], lbl[:, :], lbt.b, writes=[lbt.b])
    V("pool", lambda e: e.memset(ones[:], 1.0), W=[ones])
    V("act", lambda e: e.activation(out=gs[:, 2:3], in_=gs[:, 0:1], func=AF.Exp), R=[gs], W=[gs])
    V("dve", lambda e: e.tensor_scalar(out=gs[:, 2:3], in0=gs[:, 2:3], scalar1=-1.0, scalar2=None, op0=ALU.mult),
      R=[gs], W=[gs])
    V("dve", lambda e: e.tensor_tensor(out=lbt[:, 8:12], in0=lbt[:, 0:4], in1=lbt[:, 4:8], op=ALU.subtract),
      R=[lbt], W=[lbt])
    V("act", lambda e: e.activation(out=lbt[:, 8:12], in_=lbt[:, 8:12], func=AF.Sigmoid), R=[lbt], W=[lbt])
    V("dve", lambda e: e.tensor_scalar(out=lbt[:, 4:8], in0=lbt[:, 8:12], scalar1=-1.0, scalar2=1.0,
                                       op0=ALU.mult, op1=ALU.add), R=[lbt], W=[lbt])

    NB2 = 2
    cr = [mk("cr%d" % i, [128, 6, 132], F32, dma=True) for i in range(NB2)]
    hqf = [mk("hqf%d" % i, [128, 8, 128], F32, dma=True) for i in range(NB2)]
    vt = [mk("vt%d" % i, [128, 256], F32, dma=True) for i in range(NB2)]
    bar = [mk("bar%d" % i, [4, 2, 128], F32, dma=True) for i in range(NB2)]
    og = [mk("og%d" % i, [128, 768], BF, dma=True) for i in range(8)]
    oh = [mk("oh%d" % i, [128, 512], BF, dma=True) for i in range(8)]

    cs = mk("cs", [128, 6, 128]); sq = mk("sq", [128, 512]); rn = mk("rn", [128, 512])
    qn = mk("qn", [128, 2, 128]); kn = mk("kn", [128, 2, 128])
    qnb = mk("qnb", [128, 2, 128], BF); knb = mk("knb", [128, 2, 128], BF)
    kvt = mk("kvt", [128, 4, 128]); kkq = mk("kkq", [128, 4, 128])
    Rw = mk("Rw", [4, 3, 128]); gw = mk("gw", [4, 3, 128])
    ones4 = mk("ones4", [4, 64]); Cc = mk("Cc", [128, 32])
    V("pool", lambda e: e.memset(ones4[:], 1.0), W=[ones4])
    w = {}
    for nm in ("GrA", "Dl", "L0", "GrB", "DT", "bRS", "t1", "N0", "Q", "gam", "qd"):
        w[nm] = [mk("%s%d" % (nm, i), [128, 128]) for i in range(4)]
    NL = [[mk("N%d_%d" % (k, i), [128, 128]) for i in range(4)] for k in range(6)]
    LL = [[mk("L%d_%d" % (k, i), [128, 128]) for i in range(4)] for k in range(6)]
    rhs = [mk("rhs%d" % i, [128, 256]) for i in range(4)]
    wk = [mk("wk%d" % i, [128, 128], BF) for i in range(4)]
    colw = [mk("colw%d" % i, [128, 8]) for i in range(4)]
    hw = {}
    for nm in ("f", "lf", "kk", "Bf", "B", "E1", "E2", "EB", "EL", "keT", "nb"):
        hw[nm] = [mk("h%s%d" % (nm, i), [128, 128]) for i in range(4)]
    qm = [mk("qm%d" % i, [128, 128], BF) for i in range(4)]
    km = [mk("km%d" % i, [128, 128], BF) for i in range(4)]

    pT = mk("pT", [128, 512], F32, psum=True)
    pK = mk("pK", [128, 512], F32, psum=True)
    pS = mk("pS", [128, 512], F32, psum=True)
    pB = mk("pB", [128, 512], F32, psum=True)
    pN = [mk("pN%d" % i, [128, 512], F32, psum=True) for i in range(2)]
    pX = mk("pX", [128, 512], F32, psum=True)
    pM = mk("pM", [128, 512], F32, psum=True)
    pM1 = pS

    def load(i):
        p = i % NB2
        kb.dma("sp", cr[p][:], pc[:, 0:6, i * 128:i * 128 + 132], cr[p].b, writes=[cr[p].b])
        kb.dma("sp", hqf[p][:], pc[:, 8:16, 2 + i * 128:2 + (i + 1) * 128], hqf[p].b, writes=[hqf[p].b])
        kb.dma("sp", vt[p][:], vtok[i * 128:(i + 1) * 128, :], vt[p].b, writes=[vt[p].b])
        kb.dma("sp", bar[p][:], smallT.ap().rearrange("(a r) t -> r a t", a=2)[:, :, i * 128:(i + 1) * 128],
               bar[p].b, writes=[bar[p].b])

    itc = [0]

    def tile_(i):
        p = i % NB2
        c_, h_, v_, b_ = cr[p], hqf[p], vt[p], bar[p]
        for blk in range(6):
            en = "dve"
            V(en, lambda e, blk=blk: e.tensor_scalar(out=cs[:, blk, :], in0=c_[:, blk, 0:128],
                                                    scalar1=cw[:, blk, 0:1], scalar2=None, op0=ALU.mult),
              R=[c_, cw], W=[cs])
            for j in range(1, 5):
                V(en, lambda e, blk=blk, j=j: e.scalar_tensor_tensor(
                    out=cs[:, blk, :], in0=c_[:, blk, j:j + 128], scalar=cw[:, blk, j:j + 1], in1=cs[:, blk, :],
                    op0=ALU.mult, op1=ALU.add), R=[c_, cw, cs], W=[cs])
        if STAGE <= 1:
            return
        csf = cs[:].rearrange("p a b -> p (a b)")
        V("act", lambda e: e.activation(out=csf, in_=csf, func=AF.Silu), R=[cs], W=[cs])
        V("pool", lambda e: e.tensor_tensor(out=sq[:], in0=csf[:, 0:512], in1=csf[:, 0:512], op=ALU.mult),
          R=[cs], W=[sq])
        V("pe", lambda e: e.matmul(out=pS[:], lhsT=ones[:], rhs=sq[:], start=True, stop=True), R=[ones, sq], W=[pS])
        V("act", lambda e: e.activation(out=rn[:], in_=pS[:], func=AF.Sqrt, bias=EPS, scale=1.0), R=[pS], W=[rn])
        V("dve", lambda e: e.reciprocal(out=rn[:], in_=rn[:]), R=[rn], W=[rn])
        qnf = qn[:].rearrange("p a b -> p (a b)"); knf = kn[:].rearrange("p a b -> p (a b)")
        V("dve", lambda e: e.scalar_tensor_tensor(out=qnf, in0=csf[:, 0:256], scalar=float(128 ** -0.5),
                                                  in1=rn[:, 0:256], op0=ALU.mult, op1=ALU.mult), R=[cs, rn], W=[qn])
        V("pool", lambda e: e.tensor_tensor(out=knf, in0=csf[:, 256:512], in1=rn[:, 256:512], op=ALU.mult),
          R=[cs, rn], W=[kn])
        V("act", lambda e: e.copy(out=qnb[:].rearrange("p a b -> p (a b)"), in_=qnf), R=[qn], W=[qnb])
        V("act", lambda e: e.copy(out=knb[:].rearrange("p a b -> p (a b)"), in_=knf), R=[kn], W=[knb])
        if STAGE <= 2:
            return
        fns = []
        for j in range(2):
            fns.append(lambda e, j=j: e.transpose(out=pT[:, j * 128:(j + 1) * 128], in_=kn[:, j, :], identity=cst[:]))
        for j in range(2):
            fns.append(lambda e, j=j: e.transpose(out=pT[:, (2 + j) * 128:(3 + j) * 128], in_=cs[:, 4 + j, :],
                                                  identity=cst[:]))
        V("pe", fns, R=[kn, cs, cst], W=[pT])
        V("act", lambda e: e.copy(out=kvt[:].rearrange("p a b -> p (a b)"), in_=pT[:]), R=[pT], W=[kvt])
        fns = []
        for j in range(2):
            fns.append(lambda e, j=j: e.matmul(out=pK[:, j * 128:(j + 1) * 128], lhsT=kn[:, j, :], rhs=kn[:, j, :],
                                               start=True, stop=True))
        for j in range(2):
            fns.append(lambda e, j=j: e.matmul(out=pK[:, (2 + j) * 128:(3 + j) * 128], lhsT=knb[:, j, :],
                                               rhs=qnb[:, j, :], start=True, stop=True))
        V("pe", fns, R=[kn, knb, qnb], W=[pK])
        V("dve", lambda e: e.tensor_copy(out=kkq[:].rearrange("p a b -> p (a b)"), in_=pK[:]), R=[pK], W=[kkq])
        if STAGE <= 3:
            return
        V("act", lambda e: e.activation(out=Rw[:, 2, :], in_=b_[:, 0, :], func=AF.Sigmoid), R=[b_], W=[Rw])
        V("act", lambda e: e.activation(out=gw[:, 0, :], in_=b_[:, 1, :], func=AF.Exp, bias=gs[:, 1:2], scale=1.0),
          R=[b_, gs], W=[gw])
        V("act", lambda e: e.activation(out=gw[:, 0, :], in_=gw[:, 0, :], func=AF.Ln, bias=1.0, scale=1.0),
          R=[gw], W=[gw])
        V("dve", lambda e: e.tensor_scalar(out=gw[:, 0, :], in0=gw[:, 0, :], scalar1=gs[:, 2:3], scalar2=None,
                                           op0=ALU.mult), R=[gw, gs], W=[gw])
        for c in range(2):
            V("dve", lambda e, c=c: e.tensor_tensor_scan(out=Rw[:, 0, c * 64:(c + 1) * 64], data0=ones4[:],
                                                        data1=gw[:, 0, c * 64:(c + 1) * 64], initial=0.0,
                                                        op0=ALU.mult, op1=ALU.add), R=[gw, ones4], W=[Rw])
        V("dve", lambda e: e.tensor_tensor(out=gw[:, 1, :], in0=gw[:, 0, :], in1=Rw[:, 0, :], op=ALU.subtract),
          R=[gw, Rw], W=[gw])
        for c in range(2):
            V("dve", lambda e, c=c: e.tensor_scalar(out=Rw[:, 1, c * 64:(c + 1) * 64], in0=gw[:, 1, c * 64:(c + 1) * 64],
                                                   scalar1=Rw[:, 0, c * 64 + 63:c * 64 + 64], scalar2=None,
                                                   op0=ALU.add), R=[gw, Rw], W=[Rw])
        if STAGE <= 4:
            return
        fns = [lambda e, q=q: e.transpose(out=pB[:, 384 + q * 4:384 + (q + 1) * 4], in_=Rw[:, q, :],
                                          identity=cst[0:4, 0:4]) for q in range(3)]
        V("pe", fns, R=[Rw, cst], W=[pB])
        V("dve", lambda e: e.tensor_copy(out=Cc[:, 0:12], in_=pB[:, 384:396]), R=[pB], W=[Cc])
        V("dve", lambda e: e.tensor_scalar(out=Cc[:, 12:20], in0=Cc[:, 0:8], scalar1=-1.0, scalar2=None, op0=ALU.mult),
          R=[Cc], W=[Cc])
        V("act", lambda e: e.activation(out=Cc[:, 20:28], in_=Cc[:, 0:8], func=AF.Exp), R=[Cc], W=[Cc])

        def gdn_(dir_, h):
            if True:
                r = dir_ * 2 + h
                pp = r
                hold = False
                og_ = og[(i % 2) * 4 + r]
                ci = dir_ * 4 + r
                Gc = Cc[:, ci:ci + 1]; nGc = Cc[:, 12 + ci:13 + ci]; eGc = Cc[:, 20 + ci:21 + ci]
                bC = Cc[:, 8 + r:9 + r]
                MA = msk[:, 0 if dir_ == 0 else 1, :]
                MB = msk[:, 2 if dir_ == 0 else 3, :]
                SM = msk[:, 4 if dir_ == 0 else 5, :]
                hold = True
                V("pe", lambda e, r=r: e.matmul(out=pB[:, 0:384], lhsT=sel[:, r, :],
                                                rhs=Rw[:].rearrange("p a b -> p (a b)"), start=True, stop=True),
                  R=[sel, Rw], W=[pB])
                if not hold:
                    yield
                Gr = pB[:, dir_ * 128:(dir_ + 1) * 128]
                bR = pB[:, 256:384]
                GrA, Dl, L0, GrB, DT, bRS, t1, N0, Q, gam, qd = (w[k][pp] for k in
                    ("GrA", "Dl", "L0", "GrB", "DT", "bRS", "t1", "N0", "Q", "gam", "qd"))
                cl = colw[pp]
                if STAGE <= 6:
                    return
                V("dve", lambda e, GrA=GrA, Gr=Gr, MA=MA: e.tensor_tensor(out=GrA[:], in0=Gr, in1=MA, op=ALU.add),
                  R=[pB, msk], W=[GrA])
                if not hold:
                    yield
                V("act", lambda e, Dl=Dl, GrA=GrA, Gc=Gc: e.activation(out=Dl[:], in_=GrA[:], func=AF.Exp, bias=Gc,
                                                                    scale=-1.0), R=[GrA, Cc], W=[Dl])
                if not hold:
                    yield
                V("dve", lambda e, L0=L0, Dl=Dl, h=h, bC=bC: e.scalar_tensor_tensor(
                    out=L0[:], in0=kkq[:, h, :], scalar=bC, in1=Dl[:], op0=ALU.mult, op1=ALU.mult),
                  R=[kkq, Cc, Dl], W=[L0])
                if not hold:
                    yield
                V("dve", lambda e, GrB=GrB, Gr=Gr, MB=MB: e.tensor_tensor(out=GrB[:], in0=Gr, in1=MB, op=ALU.subtract),
                  R=[pB, msk], W=[GrB])
                if not hold:
                    yield
                V("act", lambda e, DT=DT, GrB=GrB, nGc=nGc: e.activation(out=DT[:], in_=GrB[:], func=AF.Exp, bias=nGc,
                                                                      scale=1.0), R=[GrB, Cc], W=[DT])
                if not hold:
                    yield
                V("pool", lambda e, og_=og_, DT=DT, h=h: e.tensor_tensor(out=og_[:, 128:256], in0=kkq[:, 2 + h, :],
                                                                      in1=DT[:], op=ALU.mult),
                  R=[kkq, DT], W=[og_])
                if not hold:
                    yield
                V("dve", lambda e, bRS=bRS, bR=bR, SM=SM: e.tensor_tensor(out=bRS[:], in0=bR, in1=SM, op=ALU.mult),
                  R=[pB, msk], W=[bRS])
                if not hold:
                    yield
                V("act", lambda e, gam=gam, Gr=Gr: e.activation(out=gam[:], in_=Gr, func=AF.Exp), R=[pB], W=[gam])
                if not hold:
                    yield
                if STAGE <= 7:
                    return
                c0 = 63 if dir_ == 0 else 0
                V("act", lambda e, r=r, Gr=Gr, c0=c0: e.activation(
                    out=glres[r][:, 2 * i:2 * i + 2], in_=Gr[:, c0:c0 + 65:64], func=AF.Exp), R=[pB], W=[res_b])
                if not hold:
                    yield
                if STAGE <= 7.3:
                    return
                SEL = msk[:, 8 if dir_ == 0 else 9, :]
                hold = False
                V("dve", lambda e, cl=cl, Gr=Gr, SEL=SEL, GrA=GrA: e.scalar_tensor_tensor(
                    out=GrA[:], in0=Gr, scalar=1.0, in1=SEL, op0=ALU.mult, op1=ALU.mult, accum_out=cl[:, 0:1]),
                  R=[pB, msk, Dl], W=[cl, GrA])
                if not hold:
                    yield
                if STAGE <= 7.6:
                    return
                V("pool", lambda e, t1=t1, DT=DT, h=h: e.tensor_tensor(out=t1[:], in0=kkq[:, h, :], in1=DT[:],
                                                                    op=ALU.mult), R=[kkq, DT], W=[t1])
                if not hold:
                    yield
                V("pool", lambda e, N0=N0, t1=t1, bRS=bRS: e.tensor_tensor(out=N0[:], in0=t1[:], in1=bRS[:],
                                                                        op=ALU.mult), R=[t1, bRS], W=[N0])
                if not hold:
                    yield
                V("pool", lambda e, Q=Q, N0=N0: e.tensor_tensor(out=Q[:], in0=cst[:], in1=N0[:], op=ALU.subtract),
                  R=[cst, N0], W=[Q])
                if not hold:
                    yield
                if STAGE <= 8:
                    return
                Np, Lp = N0, L0
                for k in range(1, 6):
                    pn = pN[r % 2]
                    Nk, Lk = NL[k][pp], LL[k][pp]
                    fns = []
                    if k < 5:
                        fns.append(lambda e, pn=pn, Lp=Lp, Np=Np: e.matmul(out=pn[:, 0:128], lhsT=Lp[:], rhs=Np[:],
                                                                          start=True, stop=True))
                    fns.append(lambda e, pn=pn, Lp=Lp, Np=Np: e.matmul(out=pn[:, 128:256], lhsT=Np[:], rhs=Lp[:],
                                                                      start=True, stop=True))
                    if STAGE <= 8.1:
                        return
                    V("pe", fns, R=[Lp, Np], W=[pn])
                    if not hold:
                        yield
                    if STAGE <= 8.2:
                        return
                    if k < 5 and STAGE != 8.22:
                        V("dve", lambda e, Nk=Nk, pn=pn: e.tensor_copy(out=Nk[:], in_=pn[:, 0:128]), R=[pn], W=[Nk])
                        if not hold:
                            yield
                    if STAGE == 8.21:
                        return
                    V("dve", lambda e, Lk=Lk, pn=pn: e.tensor_copy(out=Lk[:], in_=pn[:, 128:256]), R=[pn], W=[Lk])
                    if not hold:
                        yield
                    if STAGE == 8.22:
                        return
                    if STAGE <= 8.3:
                        return
                    V("pe", lambda e, pn=pn, Lk=Lk, Q=Q: e.matmul(out=pn[:, 256:384], lhsT=Lk[:], rhs=Q[:],
                                                                 start=True, stop=True), R=[Lk, Q], W=[pn])
                    if not hold:
                        yield
                    if STAGE <= 8.4:
                        return
                    V("dve", lambda e, Q=Q, pn=pn: e.tensor_tensor(out=Q[:], in0=Q[:], in1=pn[:, 256:384], op=ALU.add),
                      R=[Q, pn], W=[Q])
                    if not hold:
                        yield
                    Np, Lp = Nk, Lk
                    if STAGE <= 8.5:
                        return
                if STAGE <= 9:
                    return
                rh = rhs[pp]
                V("dve", lambda e, cl=cl, bC=bC, eGc=eGc: e.tensor_tensor(out=cl[:, 1:2], in0=bC, in1=eGc, op=ALU.mult),
                  R=[Cc], W=[cl])
                if not hold:
                    yield
                V("pool", lambda e, rh=rh, h=h, bC=bC: e.tensor_scalar(out=rh[:, 0:128], in0=kvt[:, 2 + h, :], scalar1=bC,
                                                                    scalar2=None, op0=ALU.mult), R=[kvt, Cc], W=[rh])
                if not hold:
                    yield
                V("pool", lambda e, rh=rh, h=h, cl=cl: e.tensor_scalar(out=rh[:, 128:256], in0=kvt[:, h, :],
                                                                    scalar1=cl[:, 1:2], scalar2=None, op0=ALU.mult),
                  R=[kvt, cl], W=[rh])
                if not hold:
                    yield
                hold = True
                V("pe", lambda e, Q=Q, rh=rh: e.matmul(out=pX[:, 0:256], lhsT=Q[:], rhs=rh[:], start=True, stop=True),
                  R=[Q, rh], W=[pX])
                if not hold:
                    yield
                wk_ = wk[pp]
                V("act", lambda e, og_=og_: e.copy(out=og_[:, 0:128], in_=pX[:, 0:128]), R=[pX], W=[og_])
                if not hold:
                    yield
                hold = False
                V("act", lambda e, wk_=wk_: e.copy(out=wk_[:], in_=pX[:, 128:256]), R=[pX], W=[wk_])
                if not hold:
                    yield
                V("act", lambda e, cl=cl, Gc=Gc: e.activation(out=cl[:, 2:3], in_=Gc, func=AF.Exp, bias=cl[:, 0:1],
                                                           scale=-1.0), R=[Cc, cl], W=[cl])
                if not hold:
                    yield
                V("pool", lambda e, og_=og_, h=h, cl=cl: e.tensor_scalar(out=og_[:, 256:384], in0=kvt[:, h, :],
                                                                      scalar1=cl[:, 2:3], scalar2=None, op0=ALU.mult),
                  R=[kvt, cl], W=[og_])
                if not hold:
                    yield
                if STAGE <= 10:
                    return
                hold = True
                V("pe", lambda e, wk_=wk_, og_=og_: e.matmul(out=pM[:, 0:128], lhsT=wk_[0:64, :], rhs=og_[0:64, 256:384],
                                                            start=True, stop=True), R=[wk_, og_], W=[pM])
                if not hold:
                    yield
                V("pe", lambda e, wk_=wk_, og_=og_: e.matmul(out=pM1[:, 0:128], lhsT=wk_[64:128, :],
                                                            rhs=og_[64:128, 256:384], start=True, stop=True),
                  R=[wk_, og_], W=[pM1])
                if not hold:
                    yield
                V("act", lambda e, og_=og_: e.activation(out=og_[:, 384:512], in_=pM[:, 0:128], func=AF.Copy, scale=-1.0),
                  R=[pM], W=[og_])
                if not hold:
                    yield
                hold = False
                V("act", lambda e, og_=og_: e.activation(out=og_[:, 512:640], in_=pM1[:, 0:128], func=AF.Copy,
                                                        scale=-1.0), R=[pM1], W=[og_])
                if not hold:
                    yield
                if STAGE <= 11:
                    return
                V("pool", lambda e, qd=qd, gam=gam, h=h: e.tensor_tensor(out=qd[:], in0=qn[:, h, :], in1=gam[:],
                                                                      op=ALU.mult), R=[qn, gam], W=[qd])
                if not hold:
                    yield
                hold = True
                V("pe", lambda e, wk_=wk_, og_=og_: e.matmul(out=pX[:, 256:384], lhsT=wk_[:], rhs=og_[:, 128:256],
                                                            start=True, stop=True), R=[wk_, og_], W=[pX])
                if not hold:
                    yield
                hold = False
                V("dve", lambda e, og_=og_, qd=qd: e.tensor_tensor(out=og_[:, 640:768], in0=qd[:], in1=pX[:, 256:384],
                                                                  op=ALU.subtract), R=[qd, pX], W=[og_])
                if not hold:
                    yield
                kb.dma("sp", gops[r, i, :, :], og_[:], og_.b, reads=[og_.b])
                if not hold:
                    yield


        def hgrn_(h, dir_):
            if True:
                sidx = 2 * h + dir_
                pp = sidx
                hold = False
                oh_ = oh[(i % 2) * 4 + sidx]
                f_, lf, kk, Bf, B_, E1, E2, EB, EL, keT, nb_ = (hw[k][pp] for k in
                    ("f", "lf", "kk", "Bf", "B", "E1", "E2", "EB", "EL", "keT", "nb"))
                hf_ = h_[:, 4 + sidx, :]
                hq_ = h_[:, h, :]
                V("dve", lambda e, f_=f_, hf_=hf_, sidx=sidx: e.tensor_scalar(
                    out=f_[:], in0=hf_, scalar1=lbt[:, 4 + sidx:5 + sidx], scalar2=lbt[:, 8 + sidx:9 + sidx],
                    op0=ALU.mult, op1=ALU.add), R=[h_, lbt], W=[f_])
                if not hold:
                    yield
                V("act", lambda e, lf=lf, f_=f_: e.activation(out=lf[:], in_=f_[:], func=AF.Ln), R=[f_], W=[lf])
                if not hold:
                    yield
                V("pool", lambda e, kk=kk, f_=f_: e.tensor_scalar(out=kk[:], in0=f_[:], scalar1=-1.0, scalar2=1.0,
                                                               op0=ALU.mult, op1=ALU.add), R=[f_], W=[kk])
                if not hold:
                    yield
                for c in range(2):
                    V("dve", lambda e, c=c, Bf=Bf, lf=lf: e.tensor_tensor_scan(
                        out=Bf[:, c * 64:(c + 1) * 64], data0=ones[:, 0:64], data1=lf[:, c * 64:(c + 1) * 64],
                        initial=0.0, op0=ALU.mult, op1=ALU.add), R=[lf, ones], W=[Bf])
                    if not hold:
                        yield
                if dir_ == 1:
                    V("pool", lambda e, B_=B_, lf=lf, Bf=Bf: e.tensor_tensor(out=B_[:], in0=lf[:], in1=Bf[:],
                                                                          op=ALU.subtract), R=[lf, Bf], W=[B_])
                    if not hold:
                        yield
                    for c in range(2):
                        V("pool", lambda e, c=c, B_=B_, Bf=Bf: e.tensor_scalar(
                            out=B_[:, c * 64:(c + 1) * 64], in0=B_[:, c * 64:(c + 1) * 64],
                            scalar1=Bf[:, c * 64 + 63:c * 64 + 64], scalar2=None, op0=ALU.add), R=[B_, Bf], W=[B_])
                        if not hold:
                            yield
                    Bx = B_
                    cl0 = 0
                else:
                    Bx = Bf
                    cl0 = 63
                V("pool", lambda e, nb_=nb_, Bx=Bx: e.tensor_scalar(out=nb_[:, 0:2], in0=Bx[:, 32:97:64], scalar1=-1.0,
                                                                 scalar2=None, op0=ALU.mult), R=[Bx], W=[nb_])
                if not hold:
                    yield
                for c in range(2):
                    V("act", lambda e, c=c, E1=E1, Bx=Bx, nb_=nb_: e.activation(
                        out=E1[:, c * 64:(c + 1) * 64], in_=Bx[:, c * 64:(c + 1) * 64], func=AF.Exp,
                        bias=nb_[:, c:c + 1], scale=1.0), R=[Bx, nb_], W=[E1])
                    if not hold:
                        yield
                V("dve", lambda e, E2=E2, E1=E1: e.reciprocal(out=E2[:], in_=E1[:]), R=[E1], W=[E2])
                if not hold:
                    yield
                qm_, km_ = qm[pp], km[pp]
                V("pool", lambda e, qm_=qm_, hq_=hq_, E1=E1: e.tensor_tensor(out=qm_[:], in0=hq_, in1=E1[:], op=ALU.mult),
                  R=[h_, E1], W=[qm_])
                if not hold:
                    yield
                V("pool", lambda e, km_=km_, kk=kk, E2=E2: e.tensor_tensor(out=km_[:], in0=kk[:], in1=E2[:], op=ALU.mult),
                  R=[kk, E2], W=[km_])
                if not hold:
                    yield
                V("act", lambda e, EB=EB, Bx=Bx: e.activation(out=EB[:], in_=Bx[:], func=AF.Exp), R=[Bx], W=[EB])
                if not hold:
                    yield
                V("pool", lambda e, oh_=oh_, hq_=hq_, EB=EB: e.tensor_tensor(out=oh_[:, 0:128], in0=hq_, in1=EB[:],
                                                                          op=ALU.mult), R=[h_, EB], W=[oh_])
                if not hold:
                    yield
                for c in range(2):
                    V("act", lambda e, c=c, EL=EL, Bx=Bx, cl0=cl0: e.activation(
                        out=EL[:, c * 64:(c + 1) * 64], in_=Bx[:, c * 64:(c + 1) * 64], func=AF.Exp,
                        bias=Bx[:, c * 64 + cl0:c * 64 + cl0 + 1], scale=-1.0), R=[Bx], W=[EL])
                    if not hold:
                        yield
                V("pool", lambda e, keT=keT, kk=kk, EL=EL: e.tensor_tensor(out=keT[:], in0=kk[:], in1=EL[:], op=ALU.mult),
                  R=[kk, EL], W=[keT])
                if not hold:
                    yield
                V("act", lambda e, sidx=sidx, Bx=Bx, cl0=cl0: e.activation(
                    out=glres[4 + sidx][:, 2 * i:2 * i + 2], in_=Bx[:, cl0:cl0 + 65:64], func=AF.Exp),
                  R=[Bx], W=[res_b])
                if not hold:
                    yield
                hold = True
                V("pe", lambda e, keT=keT: e.transpose(out=pM[:, 128:256], in_=keT[:], identity=cst[:]),
                  R=[keT, cst], W=[pM])
                if not hold:
                    yield
                hold = False
                V("act", lambda e, oh_=oh_: e.copy(out=oh_[:, 256:384], in_=pM[:, 128:256]), R=[pM], W=[oh_])
                if not hold:
                    yield
                hold = True
                V("pe", lambda e, km_=km_, qm_=qm_: e.matmul(out=pX[:, 384:512], lhsT=km_[:], rhs=qm_[:],
                                                            start=True, stop=True), R=[km_, qm_], W=[pX])
                if not hold:
                    yield
                M01 = msk[:, 6 if dir_ == 0 else 7, :]
                hold = False
                V("dve", lambda e, oh_=oh_, M01=M01: e.tensor_tensor(out=oh_[:, 128:256], in0=pX[:, 384:512], in1=M01,
                                                                    op=ALU.mult), R=[pX, msk], W=[oh_])
                if not hold:
                    yield
                V("pool", lambda e, oh_=oh_, h=h: e.tensor_copy(out=oh_[:, 384:512], in_=v_[:, h * 128:(h + 1) * 128]),
                  R=[v_], W=[oh_])
                if not hold:
                    yield
                kb.dma("sp", hops[sidx, i, :, :], oh_[:], oh_.b, reads=[oh_.b])
                if not hold:
                    yield

        for d_ in range(2):
            gens = [gdn_(d_, 0), gdn_(d_, 1), hgrn_(d_, 0), hgrn_(d_, 1)]
            while gens:
                for g_ in list(gens):
                    try:
                        next(g_)
                    except StopIteration:
                        gens.remove(g_)

    load(0)
    for i in range(NT):
        if i + 1 < NT:
            load(i + 1)
        tile_(i)


def phase_b(nc, kb, S, NT, gops, hops, glres, res_b, oT):
    def V(en, fn, R=(), W=()):
        kb.op(en, fn, reads=_bufs(R), writes=_bufs(W))
    mk = lambda name, shape, dt=F32, **k: TB(kb, name, shape, dt, **k)
    NS = 8
    RING = 3
    Sf = [mk("Sf%d" % s, [128, 128]) for s in range(NS)]
    Sb = [[mk("Sb%d_%d" % (s, j), [128, 128], BF) for j in range(2)] for s in range(NS)]
    ring = [[mk("rg%d_%d" % (s, j), [128, 768 if s < 4 else 512], BF, dma=True) for j in range(RING)]
            for s in range(NS)]
    stg = [[mk("stg%d_%d" % (s, j), [128, 512], F32, dma=True) for j in range(2)] for s in range(NS)]
    psS = [[mk("psS%d_%d" % (par, g), [128, 512], F32, psum=True) for g in range(2)] for par in range(2)]
    psO = [mk("psO%d" % par, [128, 512], F32, psum=True) for par in range(2)]
    for s in range(NS):
        V("pool", lambda e, s=s: e.memset(Sf[s][:], 0.0), W=[Sf[s]])
        V("pool", lambda e, s=s: e.memset(Sb[s][0][:], 0.0), W=[Sb[s][0]])

    def tile_of(s, n):
        dir_ = (s // 2) if s < 4 else (s % 2)
        k = n // 2
        return (k if dir_ == 0 else NT - 1 - k), dir_

    def load(s, k):
        ti = k if tile_of(s, 0)[1] == 0 else NT - 1 - k
        rg = ring[s][k % RING]
        src = gops[s, ti, :, :] if s < 4 else hops[s - 4, ti, :, :]
        kb.dma("sp", rg[:], src, rg.b, writes=[rg.b])

    for s in range(NS):
        load(s, 0)
        if NT > 1:
            load(s, 1)
    for n in range(2 * NT):
        par = n % 2
        for s in range(NS):
            ti, dir_ = tile_of(s, n)
            k = n // 2
            c = (n % 2) if dir_ == 0 else 1 - (n % 2)
            if n % 2 == 0 and k + 2 < NT:
                load(s, k + 2)
            rg = ring[s][k % RING]
            cur, nxt = Sb[s][n % 2], Sb[s][(n + 1) % 2]
            pS_ = psS[par][s // 4]
            pSs = pS_[:, (s % 4) * 128:(s % 4 + 1) * 128]
            pO_ = psO[par]
            pOs = pO_[:, s * 64:(s + 1) * 64]
            cr_ = slice(c * 64, (c + 1) * 64)
            if s < 4:
                V("pe", [lambda e, pSs=pSs, rg=rg, cur=cur, c=c: e.matmul(
                             out=pSs, lhsT=rg[:, 384 + c * 128:384 + (c + 1) * 128], rhs=cur[:], start=True, stop=False),
                         lambda e, pSs=pSs, rg=rg, cr_=cr_: e.matmul(
                             out=pSs, lhsT=rg[cr_, 256:384], rhs=rg[cr_, 0:128], start=False, stop=True)],
                  R=[rg, cur], W=[pS_])
                V("pe", [lambda e, pOs=pOs, rg=rg, cur=cur, c=c: e.matmul(
                             out=pOs, lhsT=cur[:], rhs=rg[:, 640 + c * 64:640 + (c + 1) * 64], start=True, stop=False),
                         lambda e, pOs=pOs, rg=rg, cr_=cr_, c=c: e.matmul(
                             out=pOs, lhsT=rg[cr_, 0:128], rhs=rg[cr_, 128 + c * 64:128 + (c + 1) * 64],
                             start=False, stop=True)],
                  R=[rg, cur], W=[pO_])
            else:
                V("pe", lambda e, pSs=pSs, rg=rg, cr_=cr_: e.matmul(
                    out=pSs, lhsT=rg[cr_, 256:384], rhs=rg[cr_, 384:512], start=True, stop=True), R=[rg], W=[pS_])
                V("pe", [lambda e, pOs=pOs, rg=rg, cur=cur, c=c: e.matmul(
                             out=pOs, lhsT=cur[:], rhs=rg[:, c * 64:(c + 1) * 64], start=True, stop=False),
                         lambda e, pOs=pOs, rg=rg, cr_=cr_, c=c: e.matmul(
                             out=pOs, lhsT=rg[cr_, 384:512], rhs=rg[cr_, 128 + c * 64:128 + (c + 1) * 64],
                             start=False, stop=True)],
                  R=[rg, cur], W=[pO_])
            gl = glres[s][:, 2 * ti + c:2 * ti + c + 1]
            V("dve", lambda e, nxt=nxt, s=s, gl=gl, pSs=pSs: e.scalar_tensor_tensor(
                out=nxt[:], in0=Sf[s][:], scalar=gl, in1=pSs, op0=ALU.mult, op1=ALU.add),
              R=[Sf[s], res_b, pS_], W=[nxt])
            V("dve", lambda e, s=s, gl=gl, pSs=pSs: e.scalar_tensor_tensor(
                out=Sf[s][:], in0=Sf[s][:], scalar=gl, in1=pSs, op0=ALU.mult, op1=ALU.add),
              R=[Sf[s], res_b, pS_], W=[Sf[s]])
            grp = ti // 4
            st_ = stg[s][grp % 2]
            pos = (ti % 4) * 128 + c * 64
            V("act", lambda e, st_=st_, pos=pos, pOs=pOs: e.copy(out=st_[:, pos:pos + 64], in_=pOs), R=[pO_], W=[st_])
            last = (ti % 4 == 3 and c == 1) if dir_ == 0 else (ti % 4 == 0 and c == 0)
            if last:
                kb.dma("sp", oT[s, :, grp * 512:(grp + 1) * 512], st_[:], st_.b, reads=[st_.b])


def phase_n(nc, kb, S, projcm, oT, gnw, hnw, y_in):
    def V(en, fn, R=(), W=()):
        kb.op(en, fn, reads=_bufs(R), writes=_bufs(W))
    mk = lambda name, shape, dt=F32, **k: TB(kb, name, shape, dt, **k)
    ones = mk("nones", [128, 128]); nw = mk("nw", [128, 2], F32, dma=True)
    V("pool", lambda e: e.memset(ones[:], 1.0), W=[ones])
    kb.dma("sp", nw[:, 0:1], gnw[:, :], nw.b, writes=[nw.b])
    kb.dma("sp", nw[:, 1:2], hnw[:, :], nw.b, writes=[nw.b])
    of = [mk("of%d" % i, [128, 512], F32, dma=True) for i in range(2)]
    ob = [mk("ob%d" % i, [128, 512], F32, dma=True) for i in range(2)]
    gt = [mk("gt%d" % i, [128, 512], F32, dma=True) for i in range(2)]
    yo = [mk("yo%d" % i, [128, 512], BF, dma=True) for i in range(2)]
    o_ = mk("no", [128, 512]); sq = mk("nsq", [128, 512]); rs = mk("nrs", [128, 512])
    ps = [mk("nps%d" % i, [128, 512], F32, psum=True) for i in range(2)]
    jobs = []
    for slot in range(4):
        if slot < 2:
            sf, sbw, gblk, wc = slot, 2 + slot, 6 + slot, 0
        else:
            h = slot - 2
            sf, sbw, gblk, wc = 4 + 2 * h, 4 + 2 * h + 1, 10 + h, 1
        for bk in range(S // 512):
            jobs.append((slot, sf, sbw, gblk, wc, bk))

    def load(j):
        slot, sf, sbw, gblk, wc, bk = jobs[j]
        p = j % 2
        kb.dma("sp", of[p][:], oT[sf, :, bk * 512:(bk + 1) * 512], of[p].b, writes=[of[p].b])
        kb.dma("sp", ob[p][:], oT[sbw, :, bk * 512:(bk + 1) * 512], ob[p].b, writes=[ob[p].b])
        kb.dma("sp", gt[p][:], projcm[gblk, :, 2 + bk * 512:2 + (bk + 1) * 512], gt[p].b, writes=[gt[p].b])

    load(0)
    for j in range(len(jobs)):
        slot, sf, sbw, gblk, wc, bk = jobs[j]
        p = j % 2
        if j + 1 < len(jobs):
            load(j + 1)
        V("pool", lambda e, p=p: e.tensor_tensor(out=o_[:], in0=of[p][:], in1=ob[p][:], op=ALU.add),
          R=[of[p], ob[p]], W=[o_])
        V("pool", lambda e: e.tensor_tensor(out=sq[:], in0=o_[:], in1=o_[:], op=ALU.mult), R=[o_], W=[sq])
        V("pe", lambda e, p=p: e.matmul(out=ps[p][:], lhsT=ones[:], rhs=sq[:], start=True, stop=True),
          R=[ones, sq], W=[ps[p]])
        V("act", lambda e, p=p: e.activation(out=rs[:], in_=ps[p][:], func=AF.Sqrt, bias=EPS, scale=1.0 / 128),
          R=[ps[p]], W=[rs])
        V("dve", lambda e: e.reciprocal(out=rs[:], in_=rs[:]), R=[rs], W=[rs])
        V("dve", lambda e, wc=wc: e.scalar_tensor_tensor(out=o_[:], in0=o_[:], scalar=nw[:, wc:wc + 1], in1=rs[:],
                                                        op0=ALU.mult, op1=ALU.mult), R=[o_, nw, rs], W=[o_])
        V("pool", lambda e, p=p: e.tensor_tensor(out=yo[p][:], in0=o_[:], in1=gt[p][:], op=ALU.mult),
          R=[o_, gt[p]], W=[yo[p]])
        kb.dma("sp", y_in[slot * 128:(slot + 1) * 128, bk * 512:(bk + 1) * 512], yo[p][:], yo[p].b, reads=[yo[p].b])


def phase_c(nc, kb, S, xq, y_all, wo_b, wg_b, wu_b, wd_b, n2w, nfw, ident, out, wflag):
    def V(en, fn, R=(), W=()):
        kb.op(en, fn, reads=_bufs(R), writes=_bufs(W))
    mk = lambda name, shape, dt=F32, **k: TB(kb, name, shape, dt, **k)
    SQ = S // 4
    NTT = SQ // 512
    T = 512
    cst = mk("cid", [128, 128], F32, dma=True)
    ones = mk("cones", [128, 128])
    nw = mk("cnw", [128, 2, DC], F32, dma=True)
    kb.dma("sp", cst[:], ident[:, :], cst.b, writes=[cst.b])
    kb.dma("sp", nw[:, 0, :], n2w[:, :], nw.b, writes=[nw.b])
    kb.dma("sp", nw[:, 1, :], nfw[:, :], nw.b, writes=[nw.b])
    V("pool", lambda e: e.memset(ones[:], 1.0), W=[ones])
    xT = mk("xT", [128, DC, T])
    yT = TB(kb, "yT", [128, DC, T], BF); yT.b = kb.dbuf("yT", sw=True)
    h2 = mk("h2", [128, DC, T], BF)
    aT = mk("aT", [128, FC, T], BF)
    xin = [mk("xin%d" % i, [128, D], F32, dma=True) for i in range(2)]
    sqt = [mk("csq%d" % i, [128, T]) for i in range(2)]
    rs = mk("crs", [128, T])
    gsb = [mk("gsb%d" % i, [128, T]) for i in range(2)]
    NW = 3
    w4 = [mk("w4_%d" % i, [128, DC * 128], BF, dma=True) for i in range(2 * NW)]
    wdr = [mk("wdr%d" % i, [128, FC * 128], BF, dma=True) for i in range(2)]
    pa = [mk("pa%d" % i, [128, 512], F32, psum=True) for i in range(8)]
    pid = {}

    def tok0(e):
        if e not in pid:
            pid[e] = e.partition_id()
        return (pid[e] % 4) * SQ

    def row0(e):
        tok0(e)
        return (pid[e] // 4) * 2048

    w4i = [0]

    for tt in range(NTT):
        kb.dma("pool", yT[:], None, yT.b, writes=[yT.b], _dyn=lambda e, tt=tt: y_all.ap()[bass.ds(row0(e), 2048), bass.ds(tok0(e) + tt * T, T)].rearrange(
                   "(k p) t -> p k t", p=128))
        for j in range(4):
            xi = xin[j % 2]
            kb.dma("sp", xi[:], xq[tt * T + j * 128:tt * T + (j + 1) * 128, :], xi.b, writes=[xi.b])
            for g4 in range(4):
                pz = pa[(j * 4 + g4) % 8]
                V("pe", [lambda e, pz=pz, xi=xi, g4=g4, q=q: e.transpose(
                    out=pz[:, q * 128:(q + 1) * 128], in_=xi[:, (g4 * 4 + q) * 128:(g4 * 4 + q + 1) * 128],
                    identity=cst[:]) for q in range(4)], R=[xi, cst], W=[pz])
                en = "act" if g4 % 2 else "dve"
                if en == "act":
                    V("act", lambda e, pz=pz, g4=g4, j=j: e.copy(
                        out=xT[:, g4 * 4:(g4 + 1) * 4, j * 128:(j + 1) * 128],
                        in_=pz[:].rearrange("p (q t) -> p q t", q=4)), R=[pz], W=[xT])
                else:
                    V("dve", lambda e, pz=pz, g4=g4, j=j: e.tensor_copy(
                        out=xT[:, g4 * 4:(g4 + 1) * 4, j * 128:(j + 1) * 128],
                        in_=pz[:].rearrange("p (q t) -> p q t", q=4)), R=[pz], W=[xT])

        def wload(src, blk):
            wt = w4[w4i[0] % (2 * NW)]
            w4i[0] += 1
            kb.dma("sp", wt[:], src[blk * 128:(blk + 1) * 128, :], wt.b, reads=[wflag], writes=[wt.b])
            return wt

        def rms(widx, dst_bf, dst_f32):
            pz = pa[7]
            for kc in range(DC):
                sq_ = sqt[kc % 2]
                V("pool", lambda e, sq_=sq_, kc=kc: e.tensor_tensor(out=sq_[:], in0=xT[:, kc, :], in1=xT[:, kc, :],
                                                                  op=ALU.mult), R=[xT], W=[sq_])
                V("pe", lambda e, sq_=sq_, kc=kc, pz=pz: e.matmul(out=pz[:], lhsT=ones[:], rhs=sq_[:], start=(kc == 0),
                                                               stop=(kc == DC - 1)), R=[ones, sq_], W=[pz])
            V("act", lambda e, pz=pz: e.activation(out=rs[:], in_=pz[:], func=AF.Sqrt, bias=EPS, scale=1.0 / D),
              R=[pz], W=[rs])
            V("dve", lambda e: e.reciprocal(out=rs[:], in_=rs[:]), R=[rs], W=[rs])
            for kc in range(DC):
                if dst_bf is not None:
                    V("dve", lambda e, kc=kc: e.scalar_tensor_tensor(
                        out=dst_bf[:, kc, :], in0=xT[:, kc, :], scalar=nw[:, widx, kc:kc + 1], in1=rs[:],
                        op0=ALU.mult, op1=ALU.mult), R=[xT, nw, rs], W=[dst_bf])
                else:
                    V("dve", lambda e, kc=kc: e.scalar_tensor_tensor(
                        out=xT[:, kc, :], in0=xT[:, kc, :], scalar=nw[:, widx, kc:kc + 1], in1=rs[:],
                        op0=ALU.mult, op1=ALU.mult), R=[xT, nw, rs], W=[xT])

        wnext = wload(wo_b, 0)
        for mo in range(16):
            wt = wnext
            if mo + 1 < 16:
                wnext = wload(wo_b, mo + 1)
            pz = pa[mo % 4]
            V("pe", [lambda e, pz=pz, wt=wt, kc=kc: e.matmul(out=pz[:], lhsT=wt[:, kc * 128:(kc + 1) * 128],
                                                            rhs=yT[:, kc, :], start=(kc == 0), stop=(kc == DC - 1))
                     for kc in range(DC)], R=[wt, yT], W=[pz])
            V("dve", lambda e, pz=pz, mo=mo: e.tensor_tensor(out=xT[:, mo, :], in0=xT[:, mo, :], in1=pz[:], op=ALU.add),
              R=[xT, pz], W=[xT])
        rms(0, h2, None)
        wgn = wload(wg_b, 0)
        wun = wload(wu_b, 0)
        for fo in range(FC):
            wg_, wu_ = wgn, wun
            if fo + 1 < FC:
                wgn = wload(wg_b, fo + 1)
                wun = wload(wu_b, fo + 1)
            pg, pu = pa[(2 * fo) % 6], pa[(2 * fo + 1) % 6]
            V("pe", [lambda e, pg=pg, wg_=wg_, kc=kc: e.matmul(out=pg[:], lhsT=wg_[:, kc * 128:(kc + 1) * 128],
                                                             rhs=h2[:, kc, :], start=(kc == 0), stop=(kc == DC - 1))
                     for kc in range(DC)], R=[wg_, h2], W=[pg])
            V("pe", [lambda e, pu=pu, wu_=wu_, kc=kc: e.matmul(out=pu[:], lhsT=wu_[:, kc * 128:(kc + 1) * 128],
                                                             rhs=h2[:, kc, :], start=(kc == 0), stop=(kc == DC - 1))
                     for kc in range(DC)], R=[wu_, h2], W=[pu])
            gs_ = gsb[fo % 2]
            V("act", lambda e, gs_=gs_, pg=pg: e.activation(out=gs_[:], in_=pg[:], func=AF.Silu), R=[pg], W=[gs_])
            V("dve", lambda e, gs_=gs_, pu=pu, fo=fo: e.tensor_tensor(out=aT[:, fo, :], in0=gs_[:], in1=pu[:],
                                                                     op=ALU.mult), R=[gs_, pu], W=[aT])
        def dload(mo):
            wt = wdr[mo % 2]
            kb.dma("sp", wt[:], wd_b[mo * 128:(mo + 1) * 128, :], wt.b, reads=[wflag], writes=[wt.b])
            return wt
        wdn = dload(0)
        for mo in range(16):
            wt = wdn
            if mo + 1 < 16:
                wdn = dload(mo + 1)
            pz = pa[mo % 4]
            V("pe", [lambda e, pz=pz, wt=wt, fc=fc: e.matmul(out=pz[:], lhsT=wt[:, fc * 128:(fc + 1) * 128],
                                                            rhs=aT[:, fc, :], start=(fc == 0), stop=(fc == FC - 1))
                     for fc in range(FC)], R=[wt, aT], W=[pz])
            V("dve", lambda e, pz=pz, mo=mo: e.tensor_tensor(out=xT[:, mo, :], in0=xT[:, mo, :], in1=pz[:], op=ALU.add),
              R=[xT, pz], W=[xT])
        rms(1, None, xT)
        for j in range(4):
            xo = xin[j % 2]
            for g4 in range(4):
                pz = pa[(j * 4 + g4) % 8]
                V("pe", [lambda e, pz=pz, g4=g4, q=q, j=j: e.transpose(
                    out=pz[:, q * 128:(q + 1) * 128], in_=xT[:, g4 * 4 + q, j * 128:(j + 1) * 128],
                    identity=cst[:]) for q in range(4)], R=[xT, cst], W=[pz])
                if g4 % 2:
                    V("act", lambda e, pz=pz, g4=g4, xo=xo: e.copy(out=xo[:, g4 * 512:(g4 + 1) * 512], in_=pz[:]),
                      R=[pz], W=[xo])
                else:
                    V("dve", lambda e, pz=pz, g4=g4, xo=xo: e.tensor_copy(out=xo[:, g4 * 512:(g4 + 1) * 512], in_=pz[:]),
                      R=[pz], W=[xo])
            kb.dma("sp", out[tt * T + j * 128:tt * T + (j + 1) * 128, :], xo[:], xo.b, reads=[xo.b])


def _col_order(hg):
    h0, h1 = 2 * hg, 2 * hg + 1
    r = lambda a: np.arange(a, a + 128)
    gq = lambda h: r(h * 128)
    gk = lambda h: r(1024 + h * 128)
    gv = lambda h: r(2048 + h * 128)
    gz = lambda h: r(3072 + h * 128)
    hq = lambda h: r(4128 + h * 128)
    hf = lambda h, d: r(5152 + d * 1024 + h * 128)
    hi = lambda h: r(7200 + h * 128)
    hgt = lambda h: r(8224 + h * 128)
    blocks = [gq(h0), gq(h1), gk(h0), gk(h1), gv(h0), gv(h1), gz(h0), gz(h1),
              hq(h0), hq(h1), hgt(h0), hgt(h1), hf(h0, 0), hf(h0, 1), hf(h1, 0), hf(h1, 1),
              hi(h0), hi(h1)]
    small = np.array([4096 + 0 * 8 + h0, 4096 + 0 * 8 + h1, 4096 + 1 * 8 + h0, 4096 + 1 * 8 + h1,
                      4112 + 0 * 8 + h0, 4112 + 0 * 8 + h1, 4112 + 1 * 8 + h0, 4112 + 1 * 8 + h1])
    return np.concatenate(blocks + [small])


def _conv_cols(hg):
    h0, h1 = 2 * hg, 2 * hg + 1
    r = lambda a: np.arange(a, a + 128)
    return [r(h0 * 128), r(h1 * 128), r(1024 + h0 * 128), r(1024 + h1 * 128), r(2048 + h0 * 128), r(2048 + h1 * 128)]


def _masks():
    i = np.arange(128)
    same = (i[:, None] // 64) == (i[None, :] // 64)
    p, f = i[:, None], i[None, :]
    BIG = 30000.0
    LS = np.where(same & (f < p), 0.0, BIG)
    US = np.where(same & (f > p), 0.0, BIG)
    UI = np.where(same & (f >= p), 0.0, BIG)
    LI = np.where(same & (f <= p), 0.0, BIG)
    US01 = (same & (f > p)).astype(np.float32)
    LS01 = (same & (f < p)).astype(np.float32)
    UI01 = (same & (f >= p)).astype(np.float32)
    LI01 = (same & (f <= p)).astype(np.float32)
    SELF = (f == (p // 64) * 64 + 63).astype(np.float32)
    SELB = (f == (p // 64) * 64).astype(np.float32)
    return np.concatenate([LS, US, UI, LI, US01, LS01, UI01, LI01, SELF, SELB], axis=1).astype(np.float32)


def make_in_maps(S, x, norm1_w, w_in, conv_w, gdn_a_log, gdn_dt_bias, gdn_norm_w, hgrn_lb_logits,
                 hgrn_norm_w, w_out, norm2_w, w_gate, w_up, w_down, norm_f_w):
    f = lambda a: np.ascontiguousarray(np.asarray(a, dtype=np.float32))
    x = f(x)
    w_in = f(w_in)[0]
    conv_w = f(conv_w)[0]
    SQ = S // 4
    vec = lambda w: f(np.asarray(w).reshape(DC, 128).T)
    ident = np.eye(128, dtype=np.float32)
    sel4 = np.zeros((4, 4 * 128), np.float32)
    for r in range(4):
        sel4[r, r * 128:(r + 1) * 128] = 1.0
    perm = []
    for r in range(4):
        for hb in (2 * r, 2 * r + 1, 8 + 2 * r, 8 + 2 * r + 1):
            perm.append(np.arange(hb * 128, (hb + 1) * 128))
    perm = np.concatenate(perm)
    wo = f(w_out)[0][perm]
    blk = lambda w, nk, nm: f(w.reshape(nk, 128, nm, 128).transpose(2, 1, 0, 3).reshape(nm * 128, nk * 128))
    wo_h = blk(wo, DC, 16)
    wg_h = blk(f(w_gate)[0], DC, FC)
    wu_h = blk(f(w_up)[0], DC, FC)
    wd_h = blk(f(w_down)[0], FC, 16)
    lbl_all = f(hgrn_lb_logits)
    maps = []
    for c in range(8):
        b, hg = c // 4, c % 4
        h0, h1 = 2 * hg, 2 * hg + 1
        cols = _col_order(hg)
        wsel = w_in[:, cols]
        win = f(wsel.reshape(DC, 128, NCOL).transpose(1, 0, 2).reshape(128, DC * NCOL))
        cw = np.stack([conv_w[:, cc].T for cc in _conv_cols(hg)], axis=1)
        gsm = np.stack([
            np.array([gdn_a_log[0][0][h0], gdn_a_log[0][0][h1], gdn_a_log[0][1][h0], gdn_a_log[0][1][h1]]),
            np.array([gdn_dt_bias[0][0][h0], gdn_dt_bias[0][0][h1], gdn_dt_bias[0][1][h0], gdn_dt_bias[0][1][h1]]),
        ], axis=1)
        lb = []
        for layer in range(2):
            for h in (h0, h1):
                for d_ in range(2):
                    lb.append(lbl_all[layer, d_, h * 128:(h + 1) * 128])
        lbl = np.stack(lb, axis=1)
        maps.append(dict(
            xb=f(x[b, :S]), xq=f(x[b, hg * SQ:(hg + 1) * SQ]), win=win,
            n1w=vec(norm1_w), n2w=vec(norm2_w), nfw=vec(norm_f_w),
            convw=f(cw.reshape(128, 30)), gsm=f(gsm), gnw=f(np.asarray(gdn_norm_w).reshape(128, 1)),
            hnw=f(np.asarray(hgrn_norm_w).reshape(128, 1)), lbl=f(lbl), ident=ident, masks=_masks(), sel4=sel4,
            wo_h=f(wo_h[c * 256:(c + 1) * 256]), wg_h=f(wg_h[c * 704:(c + 1) * 704]),
            wu_h=f(wu_h[c * 704:(c + 1) * 704]), wd_h=f(wd_h[c * 256:(c + 1) * 256])))
    return maps


_NC_CACHE = {}


def kernel(**inputs):
    S = int(np.asarray(inputs["x"]).shape[1])
    if S not in _NC_CACHE:
        _NC_CACHE[S] = build(S)
    nc = _NC_CACHE[S]
    maps = make_in_maps(S, **inputs)
    res = run_bass_kernel_spmd(nc, maps, core_ids=list(range(8)))
    SQ = S // 4
    out = np.zeros((2, S, D), np.float32)
    for c in range(8):
        b, hg = c // 4, c % 4
        out[b, hg * SQ:(hg + 1) * SQ] = res.results[c]["out"]
    return out
```
